# Optimizing a Trainium2 kernel written in Bass

```python
import math
import jax, jax.numpy as jnp
from jax import lax
import numpy as np

D_MODEL = 4096
BATCH = 4
SEQ = 4096
DEPTH = 1

GRID_W = 64
MIX_WIDTH = D_MODEL
HG_WIDTH = MIX_WIDTH // 2
NA_WIDTH = MIX_WIDTH - HG_WIDTH
HG_HEAD_DIM = 128
HG_HEADS = HG_WIDTH // HG_HEAD_DIM
NA_HEAD_DIM = 128
NA_HEADS = NA_WIDTH // NA_HEAD_DIM
NA_WIN_ROWS = 8
NA_WIN_COLS = 16
HG_CHUNK = 64
D_FF = ((8 * D_MODEL // 3 + 255) // 256) * 256
FFN_RES_WEIGHT = 0.5
EPS = 1e-6
IN_COLS = 5 * HG_WIDTH + 3 * NA_WIDTH
IN_SPLITS = [HG_WIDTH, 2 * HG_WIDTH, 3 * HG_WIDTH, 4 * HG_WIDTH, 5 * HG_WIDTH,
             5 * HG_WIDTH + NA_WIDTH, 5 * HG_WIDTH + 2 * NA_WIDTH]

kernel_name = "hybrid_hgrn2_natten_macaron_block"


def rmsnorm(x, gain):
    xf = x.astype(jnp.float32)
    y = xf * lax.rsqrt(jnp.mean(xf * xf, axis=-1, keepdims=True) + EPS)
    return (y * gain.astype(jnp.float32)).astype(x.dtype)


def swiglu(x, w_gate, w_up, w_down):
    return (jax.nn.silu(x @ w_gate) * (x @ w_up)) @ w_down


def gla_chunkwise(q, k, v, log_f):
    B, T, H, DK = q.shape
    DV = v.shape[-1]
    n_chunks = T // HG_CHUNK

    def to_chunks(a):
        return a.reshape(B, n_chunks, HG_CHUNK, H, a.shape[-1]).transpose(1, 0, 3, 2, 4)

    xs = tuple(to_chunks(a) for a in (q, k, v, log_f))
    tri = jnp.tril(jnp.ones((HG_CHUNK, HG_CHUNK), dtype=bool))

    def step(S, chunk):
        qb, kb, vb, gb = chunk
        b = jnp.cumsum(gb, axis=-2)
        diff = b[..., :, None, :] - b[..., None, :, :]
        decay = jnp.exp(jnp.where(tri[:, :, None], diff, -jnp.inf))
        scores = jnp.einsum('bhtk,bhsk,bhtsk->bhts', qb, kb, decay)
        o = (jnp.einsum('bhts,bhsv->bhtv', scores, vb)
             + jnp.einsum('bhtk,bhkv->bhtv', qb * jnp.exp(b), S))
        b_last = b[..., -1:, :]
        S = (jnp.exp(b_last[..., 0, :])[..., None] * S
             + jnp.einsum('bhsk,bhsv->bhkv', kb * jnp.exp(b_last - b), vb))
        return S, o

    S0 = jnp.zeros((B, H, DK, DV), jnp.float32)
    _, o = lax.scan(step, S0, xs)
    return o.transpose(1, 0, 3, 2, 4).reshape(B, T, H, DV)


def hgrn2_bidirectional(q_raw, f_fwd_raw, f_bwd_raw, i_raw, g_raw, lb_logits, layer, head_norm):
    B, T, _ = q_raw.shape
    out_dtype = q_raw.dtype
    lb = jnp.cumsum(jax.nn.softmax(lb_logits.astype(jnp.float32), axis=0), axis=0)[layer]

    def heads(a):
        return a.astype(jnp.float32).reshape(B, T, HG_HEADS, HG_HEAD_DIM)

    qh = heads(jax.nn.silu(q_raw))
    vh = heads(i_raw)

    def direction(f_raw, lb_d, reverse):
        f = lb_d + (1.0 - lb_d) * jax.nn.sigmoid(f_raw.astype(jnp.float32))
        fh = heads(f)
        args = (qh, 1.0 - fh, vh, jnp.log(fh))
        if reverse:
            args = tuple(jnp.flip(a, axis=1) for a in args)
        o = gla_chunkwise(*args)
        return jnp.flip(o, axis=1) if reverse else o

    o = direction(f_fwd_raw, lb[0], False) + direction(f_bwd_raw, lb[1], True)
    o = rmsnorm(o, head_norm) * jax.nn.silu(heads(g_raw))
    return o.reshape(B, T, HG_WIDTH).astype(out_dtype)


def neighbourhood_attention_2d(q, k, v, rpb):
    B, T, _ = q.shape
    rows = T // GRID_W
    kr = min(NA_WIN_ROWS, rows)
    kc = NA_WIN_COLS

    def grid(a):
        return a.reshape(B, rows, GRID_W, NA_HEADS, NA_HEAD_DIM).transpose(0, 3, 1, 2, 4)

    qg, kg, vg = grid(q), grid(k), grid(v)
    col = jnp.arange(GRID_W)
    col_start = jnp.clip(col - kc // 2, 0, GRID_W - kc)
    col_mask = (col[None, :] >= col_start[:, None]) & (col[None, :] < col_start[:, None] + kc)
    dc_idx = jnp.clip(col[None, :] - col[:, None] + NA_WIN_COLS - 1, 0, 2 * NA_WIN_COLS - 2)
    rpb32 = rpb.astype(jnp.float32)
    scale = NA_HEAD_DIM ** -0.5

    def one_row(args):
        r, q_row = args
        row_start = jnp.clip(r - kr // 2, 0, rows - kr)
        k_blk = lax.dynamic_slice_in_dim(kg, row_start, kr, axis=2)
        v_blk = lax.dynamic_slice_in_dim(vg, row_start, kr, axis=2)
        s = jnp.einsum('bhcd,bhrwd->bhcrw', q_row, k_blk).astype(jnp.float32) * scale
        dr_idx = row_start + jnp.arange(kr) - r + NA_WIN_ROWS - 1
        bias = rpb32[:, dr_idx[None, :, None], dc_idx[:, None, :]]
        s = jnp.where(col_mask[:, None, :], s + bias, -jnp.inf)
        p = jax.nn.softmax(s.reshape(B, NA_HEADS, GRID_W, kr * GRID_W), axis=-1).reshape(s.shape)
        return jnp.einsum('bhcrw,bhrwd->bhcd', p.astype(v_blk.dtype), v_blk)

    out = lax.map(one_row, (jnp.arange(rows), qg.transpose(2, 0, 1, 3, 4)))
    return out.transpose(1, 0, 3, 2, 4).reshape(B, T, NA_WIDTH)


def setup_inputs(seed: int = 0) -> dict:
    key = jax.random.key(seed)
    ks = jax.random.split(key, 20)
    f32 = jnp.float32

    def normal(k, shape, scale):
        return jax.random.normal(k, shape, f32) * scale

    def gain(k, shape):
        return 1.0 + 0.01 * jax.random.normal(k, shape, f32)

    return {
        "x": jax.random.normal(ks[0], (BATCH, SEQ, D_MODEL), f32),
        "ffn1_norm_pre": gain(ks[1], (DEPTH, D_MODEL)),
        "ffn1_w_gate": normal(ks[2], (DEPTH, D_MODEL, D_FF), D_MODEL ** -0.5),
        "ffn1_w_up": normal(ks[3], (DEPTH, D_MODEL, D_FF), D_MODEL ** -0.5),
        "ffn1_w_down": normal(ks[4], (DEPTH, D_FF, D_MODEL), D_FF ** -0.5),
        "ffn1_norm_post": gain(ks[5], (DEPTH, D_MODEL)),
        "mix_norm_pre": gain(ks[6], (DEPTH, D_MODEL)),
        "w_in": normal(ks[7], (DEPTH, D_MODEL, IN_COLS), D_MODEL ** -0.5),
        "hgrn_lb_logits": normal(ks[8], (DEPTH + 1, 2, HG_WIDTH), 1.0),
        "hgrn_head_norm": gain(ks[9], (DEPTH, HG_HEAD_DIM)),
        "na_rpb": normal(ks[10], (DEPTH, NA_HEADS, 2 * NA_WIN_ROWS - 1, 2 * NA_WIN_COLS - 1), 0.02),
        "w_out": normal(ks[11], (DEPTH, MIX_WIDTH, D_MODEL), MIX_WIDTH ** -0.5),
        "mix_norm_post": gain(ks[12], (DEPTH, D_MODEL)),
        "ffn2_norm_pre": gain(ks[13], (DEPTH, D_MODEL)),
        "ffn2_w_gate": normal(ks[14], (DEPTH, D_MODEL, D_FF), D_MODEL ** -0.5),
        "ffn2_w_up": normal(ks[15], (DEPTH, D_MODEL, D_FF), D_MODEL ** -0.5),
        "ffn2_w_down": normal(ks[16], (DEPTH, D_FF, D_MODEL), D_FF ** -0.5),
        "ffn2_norm_post": gain(ks[17], (DEPTH, D_MODEL)),
    }


def reference(x, ffn1_norm_pre, ffn1_w_gate, ffn1_w_up, ffn1_w_down, ffn1_norm_post,
              mix_norm_pre, w_in, hgrn_lb_logits, hgrn_head_norm, na_rpb, w_out, mix_norm_post,
              ffn2_norm_pre, ffn2_w_gate, ffn2_w_up, ffn2_w_down, ffn2_norm_post):
    h = x
    for layer in range(DEPTH):
        ff = swiglu(rmsnorm(h, ffn1_norm_pre[layer]), ffn1_w_gate[layer], ffn1_w_up[layer], ffn1_w_down[layer])
        h = h + FFN_RES_WEIGHT * rmsnorm(ff, ffn1_norm_post[layer])

        u = rmsnorm(h, mix_norm_pre[layer])
        proj = u @ w_in[layer]
        hq, hf_fwd, hf_bwd, hi, hg, nq, nk, nv = jnp.split(proj, IN_SPLITS, axis=-1)
        o_hg = hgrn2_bidirectional(hq, hf_fwd, hf_bwd, hi, hg, hgrn_lb_logits, layer, hgrn_head_norm[layer])
        o_na = neighbourhood_attention_2d(nq, nk, nv, na_rpb[layer])
        mixed = jnp.concatenate([o_hg, o_na], axis=-1) @ w_out[layer]
        h = h + rmsnorm(mixed, mix_norm_post[layer])

        ff = swiglu(rmsnorm(h, ffn2_norm_pre[layer]), ffn2_w_gate[layer], ffn2_w_up[layer], ffn2_w_down[layer])
        h = h + FFN_RES_WEIGHT * rmsnorm(ff, ffn2_norm_post[layer])
    return h
```

```python
import numpy as np
import ml_dtypes
from contextlib import ExitStack
import concourse.bass as bass
import concourse.mybir as mybir
from concourse.bass_utils import run_bass_kernel_spmd

F32 = mybir.dt.float32
BF16 = mybir.dt.bfloat16
AF = mybir.ActivationFunctionType
ALU = mybir.AluOpType

D = 4096
DFF = 11008
NC_ = 32
NJ = 86
T = 512
E = 4096
OWN = 2048
EPS = 1e-6
NH = 16
NEG = -30000.0


class Prog:
    ENGS = ("pe", "act", "dve", "pool", "sp")

    def __init__(self, nc):
        self.nc = nc
        self.ops = []
        self.state = {}
        self.last = {e: None for e in self.ENGS}

    def op(self, eng, fn, r=(), w=(), slot=None):
        oid = len(self.ops)
        dma = slot is not None
        deps = {}
        for k in r:
            st = self.state.get(k)
            if st and st[0] is not None:
                deps[st[0]] = "raw"
        for k in w:
            st = self.state.get(k)
            if st:
                if st[0] is not None:
                    deps.setdefault(st[0], "waw")
                lastc = {}
                for x in st[1]:
                    o = self.ops[x]
                    if o["dma"]:
                        deps.setdefault(x, "war")
                    else:
                        lastc[o["eng"]] = max(lastc.get(o["eng"], -1), x)
                for x in lastc.values():
                    deps.setdefault(x, "war")
        fdeps = set()
        for d, kind in deps.items():
            o = self.ops[d]
            if o["eng"] == eng and not o["dma"] and not dma:
                if eng == "pe" or kind != "raw":
                    continue
            fdeps.add(d)
        for k in r:
            self.state.setdefault(k, [None, []])[1].append(oid)
        for k in w:
            self.state[k] = [oid, []]
        self.ops.append(dict(eng=eng, fn=fn, deps=fdeps, dma=dma, slot=slot))
        self.last[eng] = oid
        return oid

    def barrier(self):
        ids = set(range(len(self.ops)))
        need = set()
        seen_slot = set()
        seen_eng = set()
        for oid in range(len(self.ops) - 1, -1, -1):
            o = self.ops[oid]
            if o["dma"]:
                if o["slot"] not in seen_slot:
                    seen_slot.add(o["slot"])
                    need.add(oid)
            elif o["fn"] is not None:
                if o["eng"] not in seen_eng:
                    seen_eng.add(o["eng"])
                    need.add(oid)
        for e in self.ENGS:
            self.ops.append(dict(eng=e, fn=None, deps=set(need), dma=False, slot=None))
        self.state = {}

    def emit(self, stack):
        nc = self.nc
        has_dep = set()
        for o in self.ops:
            has_dep |= o["deps"]
        eng_sem = {}
        slot_sem = {}
        eng_cnt = {e: 0 for e in self.ENGS}
        slot_cnt = {}
        ev = {}
        for oid, o in enumerate(self.ops):
            if o["fn"] is None:
                continue
            if o["dma"]:
                s = o["slot"]
                if s not in slot_sem:
                    slot_sem[s] = stack.enter_context(nc.semaphore("d_" + s))
                    slot_cnt[s] = 0
                slot_cnt[s] += 16
                ev[oid] = (slot_sem[s], slot_cnt[s])
                o["inc"] = (slot_sem[s], 16)
            elif oid in has_dep:
                e = o["eng"]
                if e not in eng_sem:
                    eng_sem[e] = stack.enter_context(nc.semaphore("e_" + e))
                eng_cnt[e] += 1
                ev[oid] = (eng_sem[e], eng_cnt[e])
                o["inc"] = (eng_sem[e], 1)
            else:
                o["inc"] = None
        self.n_sems = len(eng_sem) + len(slot_sem)
        per_eng = {e: [] for e in self.ENGS}
        for oid, o in enumerate(self.ops):
            per_eng[o["eng"]].append(o)
        block = stack.enter_context(nc.Block())

        def run(engname, eng):
            waited = {}
            for o in per_eng[engname]:
                waits = {}
                for d in o["deps"]:
                    sem, val = ev[d]
                    key = id(sem)
                    if waits.get(key, (None, 0))[1] < val:
                        waits[key] = (sem, val)
                for key, (sem, val) in waits.items():
                    if waited.get(key, 0) >= val:
                        continue
                    waited[key] = val
                    eng.wait_ge(sem, val)
                if o["fn"] is None:
                    continue
                ins = o["fn"](eng)
                if o["inc"] is not None:
                    ins.then_inc(o["inc"][0], o["inc"][1])

        @block.tensor
        def _(e):
            run("pe", e)

        @block.scalar
        def _(e):
            run("act", e)

        @block.vector
        def _(e):
            run("dve", e)

        @block.gpsimd
        def _(e):
            run("pool", e)

        @block.sync
        def _(e):
            run("sp", e)


def build_program(dbg=(), phases=6, n_ffn1_tiles=8):
    nc = bass.Bass("TRN2", target_bir_lowering=False)

    def din(name, shape, dt=F32):
        return nc.dram_tensor(name, list(shape), dt, kind="ExternalInput").ap()

    def dscr(name, shape, dt=F32):
        kind = "ExternalOutput" if name in dbg else "Internal"
        return nc.dram_tensor(name, list(shape), dt, kind=kind).ap()

    xT = din("xT", [D, E])
    gains = din("gains", [128, 6 * NC_])
    wg = [din("wg1", [NJ, 128, NC_ * 128]), din("wg2", [NJ, 128, NC_ * 128])]
    wu = [din("wu1", [NJ, 128, NC_ * 128]), din("wu2", [NJ, 128, NC_ * 128])]
    wd = [din("wd1", [NC_, 2, 128, 43 * 128]), din("wd2", [NC_, 2, 128, 43 * 128])]
    win_fm = din("win_fm", [96, 128, NC_ * 128])
    win_tm = din("win_tm", [8, 128, NC_ * 512])
    wout = din("wout", [NC_, 128, NC_ * 128])
    lbl = din("lbl", [128, 64])
    hn = din("hn", [128, 1])
    strip = din("strip", [NH, 128, 1408])
    maskd = din("mask", [16, 128, 512])
    identd = din("ident", [128, 128])
    trid = din("tri", [64, 128])
    outT = nc.dram_tensor("outT", [D, OWN], F32, kind="ExternalOutput").ap()

    h1T = dscr("h1T", [D, E])
    ffT = dscr("ffT", [D, E])
    h2T = dscr("h2T", [D, OWN])
    qT_s = dscr("qT_s", [NH, 128, OWN])
    fAT_s = dscr("fAT_s", [NH, 128, OWN])
    fBT_s = dscr("fBT_s", [NH, 128, E])
    gT_s = dscr("gT_s", [NH, 128, OWN])
    v_s = dscr("v_s", [E, 2048], BF16)
    nqT_s = dscr("nqT_s", [NH, 128, OWN], BF16)
    nkT_s = dscr("nkT_s", [NH, 128, OWN + T], BF16)
    nv_s = dscr("nv_s", [OWN + T, 2048], BF16)
    mixT_s = dscr("mixT_s", [D, OWN], BF16)

    P = Prog(nc)
    stack = ExitStack()
    arena = stack.enter_context(nc.sbuf_tensor("arena", [128, 53000], F32))
    psum = stack.enter_context(nc.psum_tensor("psum", [128, 4096], F32))

    class Carver:
        def __init__(self):
            self.off = 0

        def f32(self, n):
            a = arena[:, self.off:self.off + n]
            self.off += n
            assert self.off <= 53000, self.off
            return a

        def bf16(self, n):
            assert n % 2 == 0
            return self.f32(n // 2).bitcast(BF16)

    def bank(b):
        return psum[:, b * 512:(b + 1) * 512]

    cv = Carver()
    gains_sb = cv.f32(6 * NC_)
    ones_bf = cv.bf16(128)
    ident_bf = cv.bf16(128)
    const_end = cv.off

    P.op("sp", lambda e: e.dma_start(out=gains_sb, in_=gains), w=["gains"], slot="c0")
    P.op("pool", lambda e: e.memset(ones_bf, 1.0), w=["ones"])
    P.op("pool", lambda e: e.dma_start(out=ident_bf, in_=identd), w=["ident"], slot="c1")

    def gain_ap(n, c):
        return gains_sb[:, n * NC_ + c:n * NC_ + c + 1]

    def rstd_from_bank(ps_n, rstd, n_feat, tag):
        P.op("dve", lambda e: e.tensor_scalar(rstd, ps_n, 1.0 / n_feat, EPS, ALU.mult, ALU.add),
             r=[("ps", tag)], w=[("rstd", tag)])
        P.op("act", lambda e: e.activation(rstd, rstd, AF.Sqrt), r=[("rstd", tag)], w=[("rstd", tag)])
        P.op("dve", lambda e: e.reciprocal(rstd, rstd), r=[("rstd", tag)], w=[("rstd", tag)])

    def norm_load(src, col0, gi, xn, stages, sqbs, rstd, ps_n, src_key):
        ns = len(stages)
        for c in range(NC_):
            st = stages[c % ns]
            sq = sqbs[c % 2]
            P.op("sp", lambda e, st=st, c=c: e.dma_start(out=st, in_=src[c * 128:(c + 1) * 128, col0:col0 + T]),
                 r=[src_key], w=[("stage", c % ns)], slot="st%d" % (c % ns))
            P.op("act", lambda e, st=st, sq=sq: e.activation(sq, st, AF.Square),
                 r=[("stage", c % ns)], w=[("sqb", c % 2)])
            P.op("pe", lambda e, sq=sq, c=c: e.matmul(ps_n, ones_bf, sq, start=(c == 0), stop=(c == NC_ - 1)),
                 r=[("sqb", c % 2), "ones"], w=[("ps", "n")])
        rstd_from_bank(ps_n, rstd, D, "n")
        for c in range(NC_):
            st = stages[c % ns]
            P.op("sp", lambda e, st=st, c=c: e.dma_start(out=st, in_=src[c * 128:(c + 1) * 128, col0:col0 + T]),
                 r=[src_key], w=[("stage", c % ns)], slot="st%d" % (c % ns))
            P.op("dve", lambda e, st=st, c=c: e.scalar_tensor_tensor(xn[:, c, :], st, gain_ap(gi, c), rstd,
                                                                   ALU.mult, ALU.mult),
                 r=[("stage", c % ns), ("rstd", "n"), "gains"], w=[("xn", c)])

    def finalize(col0, res_src, res_col0, res_key, gi, weight, dst, dst_col0, dst_key,
                 stages, stages2, rstd, ps_n2):
        rstd_from_bank(ps_n2, rstd, D, "n2")
        ns = len(stages)
        for m in range(NC_):
            sa = stages[m % ns]
            sb = stages2[m % 2]
            P.op("sp", lambda e, sa=sa, m=m: e.dma_start(out=sa, in_=ffT[m * 128:(m + 1) * 128, col0:col0 + T]),
                 r=[("ffT", m)], w=[("stage", m % ns)], slot="st%d" % (m % ns))
            P.op("sp", lambda e, sb=sb, m=m: e.dma_start(out=sb, in_=res_src[m * 128:(m + 1) * 128,
                                                                            res_col0:res_col0 + T]),
                 r=[res_key], w=[("stage2", m % 2)], slot="sr%d" % (m % 2))
            P.op("dve", lambda e, sa=sa, m=m: e.scalar_tensor_tensor(sa, sa, gain_ap(gi, m), rstd,
                                                                   ALU.mult, ALU.mult),
                 r=[("stage", m % ns), ("rstd", "n2"), "gains"], w=[("stage", m % ns)])
            P.op("dve", lambda e, sa=sa, sb=sb: e.scalar_tensor_tensor(sa, sa, float(weight), sb,
                                                                     ALU.mult, ALU.add),
                 r=[("stage", m % ns), ("stage2", m % 2)], w=[("stage", m % ns)])
            P.op("sp", lambda e, sa=sa, m=m: e.dma_start(out=dst[m * 128:(m + 1) * 128, dst_col0:dst_col0 + T],
                                                         in_=sa),
                 r=[("stage", m % ns)], w=[dst_key], slot="so%d" % (m % ns))

    def ff_chunk_out(ps_o, pkey, m, col0, ffst, sqb, ps_n2):
        fs = ffst[m % 2]
        sq = sqb[m % 2]
        P.op("act", lambda e, fs=fs: e.activation(fs, ps_o, AF.Copy), r=[pkey], w=[("ffst", m % 2)])
        P.op("act", lambda e, sq=sq: e.activation(sq, ps_o, AF.Square), r=[pkey], w=[("sqb", m % 2)])
        P.op("sp", lambda e, fs=fs, m=m: e.dma_start(out=ffT[m * 128:(m + 1) * 128, col0:col0 + T], in_=fs),
             r=[("ffst", m % 2)], w=[("ffT", m)], slot="sf%d" % (m % 2))
        P.op("pe", lambda e, sq=sq, m=m: e.matmul(ps_n2, ones_bf, sq, start=(m == 0), stop=(m == NC_ - 1)),
             r=[("sqb", m % 2), "ones"], w=[("ps", "n2")])

    def ffn_phase(which, src, src_key, col_tiles, gi_pre, gi_post, dst, dst_key, dst_cols):
        cv = Carver()
        cv.off = const_end
        xn = cv.bf16(NC_ * T).rearrange("p (c t) -> p c t", t=T)
        hid = cv.bf16(NJ * T).rearrange("p (c t) -> p c t", t=T)
        wgb = [cv.bf16(NC_ * 128) for _ in range(2)]
        wub = [cv.bf16(NC_ * 128) for _ in range(2)]
        wdb = [cv.bf16(43 * 128) for _ in range(2)]
        stN = [cv.f32(T) for _ in range(3)]
        sqN = [cv.bf16(T) for _ in range(4)]
        rstdN = cv.f32(T)
        ffst = [cv.f32(T) for _ in range(2)]
        sqF = [cv.bf16(T) for _ in range(2)]
        rstdF = cv.f32(T)
        stF = [cv.f32(T) for _ in range(2)]
        stR = [cv.f32(T) for _ in range(2)]
        sg = [cv.f32(T) for _ in range(2)]
        ps_g = [bank(0), bank(1)]
        ps_u = [bank(2), bank(3)]
        ps_o = [bank(4), bank(5)]
        ps_n = bank(6)
        ps_n2 = bank(7)

        def norm_steps(col0):
            steps = [[] for _ in range(34)]

            def l1(c):
                st, sq = stN[c % 3], sqN[c % 4]
                P.op("sp", lambda e: e.dma_start(out=st, in_=src[c * 128:(c + 1) * 128, col0:col0 + T]),
                     r=[src_key], w=[("stN", c % 3)], slot="sn%d" % (c % 3))
                P.op("act", lambda e: e.activation(sq, st, AF.Square), r=[("stN", c % 3)], w=[("sqN", c % 4)])

            def m1(c):
                sq = sqN[c % 4]
                P.op("pe", lambda e: e.matmul(ps_n, ones_bf, sq, start=(c == 0), stop=(c == NC_ - 1)),
                     r=[("sqN", c % 4), "ones"], w=[("ps", "n")])

            def l2(c):
                st = stN[c % 3]
                P.op("sp", lambda e: e.dma_start(out=st, in_=src[c * 128:(c + 1) * 128, col0:col0 + T]),
                     r=[src_key], w=[("stN", c % 3)], slot="sn%d" % (c % 3))
                P.op("dve", lambda e: e.scalar_tensor_tensor(xn[:, c, :], st, gain_ap(gi_pre, c), rstdN,
                                                             ALU.mult, ALU.mult),
                     r=[("stN", c % 3), ("rstd", "n"), "gains"], w=[("xn", c)])

            for c in range(NC_):
                steps[c // 2].append(lambda c=c: l1(c))
                steps[c // 2 + 1].append(lambda c=c: m1(c))
            steps[17].append(lambda: rstd_from_bank(ps_n, rstdN, D, "n"))
            for c in range(NC_):
                steps[18 + (c * 14) // NC_].append(lambda c=c: l2(c))
            return steps

        def fin_steps(col0, dcol0):
            steps = [[] for _ in range(NJ + 2)]

            def fa(m):
                sa, sb = stF[m % 2], stR[m % 2]
                P.op("sp", lambda e: e.dma_start(out=sa, in_=ffT[m * 128:(m + 1) * 128, col0:col0 + T]),
                     r=[("ffT", m)], w=[("stF", m % 2)], slot="sfa%d" % (m % 2))
                P.op("sp", lambda e: e.dma_start(out=sb, in_=src[m * 128:(m + 1) * 128, col0:col0 + T]),
                     r=[src_key], w=[("stR", m % 2)], slot="sfr%d" % (m % 2))

            def fb(m):
                sa, sb = stF[m % 2], stR[m % 2]
                P.op("dve", lambda e: e.scalar_tensor_tensor(sa, sa, gain_ap(gi_post, m), rstdF, ALU.mult, ALU.mult),
                     r=[("stF", m % 2), ("rstd", "n2"), "gains"], w=[("stF", m % 2)])
                P.op("dve", lambda e: e.scalar_tensor_tensor(sa, sa, 0.5, sb, ALU.mult, ALU.add),
                     r=[("stF", m % 2), ("stR", m % 2)], w=[("stF", m % 2)])
                P.op("sp", lambda e: e.dma_start(out=dst[m * 128:(m + 1) * 128, dcol0:dcol0 + T], in_=sa),
                     r=[("stF", m % 2)], w=[dst_key], slot="sfo%d" % (m % 2))

            for m in range(NC_):
                steps[2 + m].append(lambda m=m: fa(m))
                steps[3 + m].append(lambda m=m: fb(m))
            return steps

        def run_steps(steps, k0, k1):
            for k in range(k0, k1):
                if k < len(steps):
                    for th in steps[k]:
                        th()

        nt = len(col_tiles)
        ns = norm_steps(col_tiles[0])
        run_steps(ns, 0, len(ns))
        pend_fin = None
        for ti, col0 in enumerate(col_tiles):
            for j in range(NJ):
                b = j % 2
                P.op("pool", lambda e, b=b, j=j: e.dma_start(out=wgb[b], in_=wg[which][j]), w=[("wg", b)],
                     slot="wg%d" % b)
                P.op("pool", lambda e, b=b, j=j: e.dma_start(out=wub[b], in_=wu[which][j]), w=[("wu", b)],
                     slot="wu%d" % b)
                for c in range(NC_):
                    P.op("pe", lambda e, b=b, c=c: e.matmul(ps_g[b], wgb[b][:, c * 128:(c + 1) * 128], xn[:, c, :],
                                                            start=(c == 0), stop=(c == NC_ - 1)),
                         r=[("wg", b), ("xn", c)], w=[("ps", "g", b)])
                for c in range(NC_):
                    P.op("pe", lambda e, b=b, c=c: e.matmul(ps_u[b], wub[b][:, c * 128:(c + 1) * 128], xn[:, c, :],
                                                            start=(c == 0), stop=(c == NC_ - 1)),
                         r=[("wu", b), ("xn", c)], w=[("ps", "u", b)])
                P.op("act", lambda e, b=b: e.activation(sg[b], ps_g[b], AF.Silu), r=[("ps", "g", b)], w=[("sg", b)])
                P.op("dve", lambda e, b=b, j=j: e.tensor_tensor(hid[:, j, :], sg[b], ps_u[b], ALU.mult),
                     r=[("sg", b), ("ps", "u", b)], w=[("hid", j)])
                if pend_fin is not None:
                    run_steps(pend_fin, j, j + 1)
            if pend_fin is not None:
                run_steps(pend_fin, NJ, len(pend_fin))
                pend_fin = None
            ns = norm_steps(col_tiles[ti + 1]) if ti + 1 < nt else None
            wdk = 0
            for m in range(NC_):
                pb = m % 2
                for half in range(2):
                    k = wdk % 2
                    wdk += 1
                    P.op("pool", lambda e, k=k, m=m, half=half: e.dma_start(out=wdb[k], in_=wd[which][m, half]),
                         w=[("wd", k)], slot="wd%d" % k)
                    for jj in range(43):
                        j = half * 43 + jj
                        P.op("pe", lambda e, k=k, jj=jj, j=j, pb=pb: e.matmul(
                            ps_o[pb], wdb[k][:, jj * 128:(jj + 1) * 128], hid[:, j, :],
                            start=(j == 0), stop=(j == NJ - 1)),
                            r=[("wd", k), ("hid", j)], w=[("ps", "o", pb)])
                ff_chunk_out(ps_o[pb], ("ps", "o", pb), m, col0, ffst, sqF, ps_n2)
                if ns is not None:
                    run_steps(ns, m, m + 1)
            if ns is not None:
                run_steps(ns, NC_, len(ns))
            rstd_from_bank(ps_n2, rstdF, D, "n2")
            pend_fin = fin_steps(col0, dst_cols[ti])
        run_steps(pend_fin, 0, len(pend_fin))

    ffn_phase(0, xT, "xT", [i * T for i in range(n_ffn1_tiles)], 0, 1, h1T, "h1T",
              [i * T for i in range(n_ffn1_tiles)])
    P.barrier()


    def inproj_phase(n_tiles=8):
        cv = Carver()
        cv.off = const_end
        ubuf = [cv.bf16(NC_ * T).rearrange("p (c t) -> p c t", t=T) for _ in range(2)]
        wfm = [cv.bf16(NC_ * 128) for _ in range(3)]
        wtm = [cv.bf16(NC_ * 512) for _ in range(2)]
        stN = [cv.f32(T) for _ in range(3)]
        sqN = [cv.bf16(T) for _ in range(4)]
        rstdN = cv.f32(T)

        def norm_steps(col0, ub):
            steps = [[] for _ in range(34)]
            xn = ubuf[ub]

            def l1(c):
                st, sq = stN[c % 3], sqN[c % 4]
                P.op("sp", lambda e: e.dma_start(out=st, in_=h1T[c * 128:(c + 1) * 128, col0:col0 + T]),
                     r=["h1T"], w=[("stN", c % 3)], slot="sn%d" % (c % 3))
                P.op("act", lambda e: e.activation(sq, st, AF.Square), r=[("stN", c % 3)], w=[("sqN", c % 4)])

            def m1(c):
                sq = sqN[c % 4]
                P.op("pe", lambda e: e.matmul(ps_n, ones_bf, sq, start=(c == 0), stop=(c == NC_ - 1)),
                     r=[("sqN", c % 4), "ones"], w=[("ps", "n")])

            def l2(c):
                st = stN[c % 3]
                P.op("sp", lambda e: e.dma_start(out=st, in_=h1T[c * 128:(c + 1) * 128, col0:col0 + T]),
                     r=["h1T"], w=[("stN", c % 3)], slot="sn%d" % (c % 3))
                P.op("dve", lambda e: e.scalar_tensor_tensor(xn[:, c, :], st, gain_ap(2, c), rstdN,
                                                             ALU.mult, ALU.mult),
                     r=[("stN", c % 3), ("rstd", "n"), "gains"], w=[("u", ub, c)])

            for c in range(NC_):
                steps[c // 2].append(lambda c=c: l1(c))
                steps[c // 2 + 1].append(lambda c=c: m1(c))
            steps[17].append(lambda: rstd_from_bank(ps_n, rstdN, D, "n"))
            for c in range(NC_):
                steps[18 + (c * 14) // NC_].append(lambda c=c: l2(c))
            return steps

        side = dict(steps=None, done=0)

        def side_progress(idx, n_main):
            if side["steps"] is None:
                return
            upto = min(len(side["steps"]), ((idx + 1) * len(side["steps"]) + n_main - 1) // n_main)
            for k in range(side["done"], upto):
                for th in side["steps"][k]:
                    th()
            side["done"] = max(side["done"], upto)
        ost = [cv.f32(T) for _ in range(3)]
        obst = [cv.bf16(T) for _ in range(3)]
        pss = [bank(i) for i in range(4)]
        ps_n = bank(6)
        cnt = dict(fm=0, tm=0, ps=0, os=0, ob=0)
        qscale = float(128.0 ** -0.5)
        for th_l in norm_steps(0, 0):
            for th in th_l:
                th()
        for ti in range(n_tiles):
            col0 = ti * T
            own = ti < 4
            ub = ti % 2
            u = ubuf[ub]
            if own:
                groups = [0, 1, 2, 3, 4, 5]
            elif ti == 4:
                groups = [2, 5]
            else:
                groups = [2]
            n_tm = 8 if (own or ti == 4) else 4
            n_main = len(groups) * NH + n_tm * 4
            main_idx = 0
            if ti + 1 < n_tiles:
                side["steps"] = norm_steps((ti + 1) * T, 1 - ub)
                side["done"] = 0
            else:
                side["steps"] = None
            for G in groups:
                for h in range(NH):
                    k = cnt["fm"] % 3
                    cnt["fm"] += 1
                    pb = cnt["ps"] % 4
                    cnt["ps"] += 1
                    P.op("pool", lambda e, k=k, G=G, h=h: e.dma_start(out=wfm[k], in_=win_fm[G * 16 + h]),
                         w=[("wfm", k)], slot="wfm%d" % k)
                    for c in range(NC_):
                        P.op("pe", lambda e, k=k, c=c, pb=pb, u=u: e.matmul(pss[pb], wfm[k][:, c * 128:(c + 1) * 128],
                                                                          u[:, c, :], start=(c == 0),
                                                                          stop=(c == NC_ - 1)),
                             r=[("wfm", k), ("u", ub, c)], w=[("ps", pb)])
                    if G in (0, 1, 2, 3):
                        so = cnt["os"] % 3
                        cnt["os"] += 1
                        dst = (qT_s, fAT_s, fBT_s, gT_s)[G]
                        if G in (0, 3):
                            P.op("act", lambda e, so=so, pb=pb: e.activation(ost[so], pss[pb], AF.Silu),
                                 r=[("ps", pb)], w=[("ost", so)])
                        else:
                            P.op("dve", lambda e, so=so, pb=pb: e.tensor_copy(ost[so], pss[pb]),
                                 r=[("ps", pb)], w=[("ost", so)])
                        P.op("sp", lambda e, so=so, dst=dst, h=h, col0=col0: e.dma_start(
                            out=dst[h, :, col0:col0 + T], in_=ost[so]),
                            r=[("ost", so)], w=[("proj", G, h)], slot="os%d" % so)
                    else:
                        so = cnt["ob"] % 3
                        cnt["ob"] += 1
                        dst = nqT_s if G == 4 else nkT_s
                        if G == 4:
                            P.op("dve", lambda e, so=so, pb=pb: e.tensor_scalar_mul(obst[so], pss[pb], qscale),
                                 r=[("ps", pb)], w=[("obst", so)])
                        else:
                            P.op("dve", lambda e, so=so, pb=pb: e.tensor_copy(obst[so], pss[pb]),
                                 r=[("ps", pb)], w=[("obst", so)])
                        P.op("sp", lambda e, so=so, dst=dst, h=h, col0=col0: e.dma_start(
                            out=dst[h, :, col0:col0 + T], in_=obst[so]),
                            r=[("obst", so)], w=[("proj", G, h)], slot="ob%d" % so)
                    side_progress(main_idx, n_main)
                    main_idx += 1
            tm_groups = list(range(4)) + (list(range(4, 8)) if (own or ti == 4) else [])
            for gi in tm_groups:
                k = cnt["tm"] % 2
                cnt["tm"] += 1
                P.op("pool", lambda e, k=k, gi=gi: e.dma_start(out=wtm[k], in_=win_tm[gi]),
                     w=[("wtm", k)], slot="wtm%d" % k)
                for sub in range(4):
                    pb = cnt["ps"] % 4
                    cnt["ps"] += 1
                    for c in range(NC_):
                        P.op("pe", lambda e, k=k, c=c, pb=pb, sub=sub, u=u: e.matmul(
                            pss[pb], u[:, c, sub * 128:(sub + 1) * 128], wtm[k][:, c * 512:(c + 1) * 512],
                            start=(c == 0), stop=(c == NC_ - 1)),
                            r=[("wtm", k), ("u", ub, c)], w=[("ps", pb)])
                    so = cnt["ob"] % 3
                    cnt["ob"] += 1
                    P.op("dve", lambda e, so=so, pb=pb: e.tensor_copy(obst[so], pss[pb]),
                         r=[("ps", pb)], w=[("obst", so)])
                    r0 = col0 + sub * 128
                    if gi < 4:
                        dsta = v_s[r0:r0 + 128, gi * 512:(gi + 1) * 512]
                        key = ("v_s", gi)
                    else:
                        dsta = nv_s[r0:r0 + 128, (gi - 4) * 512:(gi - 3) * 512]
                        key = ("nv_s", gi)
                    P.op("sp", lambda e, so=so, dsta=dsta: e.dma_start(out=dsta, in_=obst[so]),
                         r=[("obst", so)], w=[key], slot="ob%d" % so)
                    side_progress(main_idx, n_main)
                    main_idx += 1
            side_progress(n_main, n_main)

    def hgrn_phase(heads=range(NH)):
        cv = Carver()
        cv.off = const_end
        XB = cv.f32(E)
        KB = cv.f32(E)
        CB = cv.f32(E)
        XA = cv.f32(OWN)
        KA = cv.f32(OWN)
        CA = cv.f32(OWN)
        cmask = cv.f32(E)
        q = cv.f32(OWN)
        g = cv.f32(OWN)
        oA = cv.f32(OWN)
        oB = cv.f32(OWN)
        rstdh = KA
        vtm = [cv.bf16(32 * 128)[0:64, :].rearrange("p (n c) -> p n c", c=128) for _ in range(2)]
        vtmO = [cv.bf16(16 * 128).rearrange("p (n c) -> p n c", c=128) for _ in range(2)]
        ktm = cv.bf16(64 * 128)[0:64, :].rearrange("p (n c) -> p n c", c=128)
        ktmO = cv.bf16(16 * 128).rearrange("p (n c) -> p n c", c=128)
        qtA = cv.bf16(OWN)
        ktA = cv.bf16(OWN)
        qtB = cv.bf16(OWN)
        ktB = cv.bf16(E)
        sqh = cv.bf16(OWN)
        ob = cv.bf16(OWN)
        dl = [cv.f32(32), cv.f32(64)]
        lraw = cv.f32(64)
        lb = cv.f32(32)
        omlb = cv.f32(32)
        S32 = [cv.f32(128), cv.f32(128)]
        Ssum = [cv.f32(128), cv.f32(128)]
        Sbf = [cv.bf16(128), cv.bf16(128)]
        scm = [cv.bf16(64) for _ in range(4)]
        tri = cv.f32(128)
        hn_sb = cv.f32(2)
        ps_n = bank(6)

        def bk(b):
            return ("ps", "b%d" % b)

        P.op("sp", lambda e: e.dma_start(out=lraw, in_=lbl), w=["lraw"], slot="hl")
        P.op("sp", lambda e: e.dma_start(out=tri[0:64, :], in_=trid), w=["tri"], slot="htri")
        P.op("sp", lambda e: e.dma_start(out=hn_sb[:, 0:1], in_=hn), w=["hn"], slot="hhn")
        P.op("dve", lambda e: e.tensor_tensor(lb, lraw[:, 0:32], lraw[:, 32:64], ALU.subtract), r=["lraw"], w=["lb"])
        P.op("act", lambda e: e.activation(lb, lb, AF.Sigmoid), r=["lb"], w=["lb"])
        P.op("dve", lambda e: e.tensor_scalar(omlb, lb, -1.0, 1.0, ALU.mult, ALU.add), r=["lb"], w=["omlb"])
        P.op("pool", lambda e: e.memset(cmask, 1.0), w=["cmask"])
        P.op("pool", lambda e: e.memset(cmask[:, 0:OWN].rearrange("p (n t) -> p n t", t=64)[:, :, 0:1], 0.0),
             w=["cmask"])
        P.op("pool", lambda e: e.memset(cmask[:, OWN:OWN + 1], 0.0), w=["cmask"])

        def loads(h):
            hb = h % 2
            P.op("sp", lambda e: e.dma_start(out=XB, in_=fBT_s[h]), r=[("proj", 2, h)], w=[("X", "B")], slot="hXB")
            P.op("sp", lambda e: e.dma_start(out=XA, in_=fAT_s[h]), r=[("proj", 1, h)], w=[("X", "A")], slot="hXA")
            P.op("sp", lambda e: e.dma_start(out=q, in_=qT_s[h]), r=[("proj", 0, h)], w=["q"], slot="hq")
            vsrc = v_s[0:OWN, h * 128:(h + 1) * 128].rearrange("(n p) c -> p n c", p=64)
            for i in range(2):
                P.op("sp", lambda e, i=i: e.dma_start(out=vtm[hb][:, i * 16:(i + 1) * 16, :],
                                                      in_=vsrc[:, i * 16:(i + 1) * 16, :]),
                     r=[("v_s", h // 4)], w=[("vtm", hb, i)], slot="hv%d_%d" % (hb, i))
            vsrcO = v_s[OWN:E, h * 128:(h + 1) * 128].rearrange("(n p) c -> p n c", p=128)
            for i in range(2):
                P.op("sp", lambda e, i=i: e.dma_start(out=vtmO[hb][:, i * 8:(i + 1) * 8, :],
                                                      in_=vsrcO[:, i * 8:(i + 1) * 8, :]),
                     r=[("v_s", h // 4)], w=[("vtmO", hb, i)], slot="hvo%d_%d" % (hb, i))

        def chain_ops(X, K, C, n, di, h, p):
            lb_ap = lb[:, di * 16 + h:di * 16 + h + 1]
            om_ap = omlb[:, di * 16 + h:di * 16 + h + 1]
            C3 = C.rearrange("p (n t) -> p n t", t=64)
            kx, kk, kc = ("X", p), ("K", p), ("C", p)
            return [
                lambda: P.op("act", lambda e: e.activation(X[:, :n], X[:, :n], AF.Sigmoid), r=[kx], w=[kx]),
                lambda: P.op("dve", lambda e: e.tensor_scalar(X[:, :n], X[:, :n], om_ap, lb_ap, ALU.mult, ALU.add),
                             r=[kx, "lb", "omlb"], w=[kx]),
                lambda: P.op("dve", lambda e: e.tensor_scalar(K[:, :n], X[:, :n], -1.0, 1.0, ALU.mult, ALU.add),
                             r=[kx], w=[kk, ("rstd", "b6")]),
                lambda: P.op("act", lambda e: e.activation(X[:, :n], X[:, :n], AF.Ln), r=[kx], w=[kx]),
                lambda: P.op("dve", lambda e: e.tensor_tensor_scan(C[:, :n], cmask[:, :n], X[:, :n], 0.0,
                                                                   ALU.mult, ALU.add), r=[kx, "cmask"], w=[kc]),
                lambda: P.op("act", lambda e: e.activation(dl[di][:, 0:32], C3[:, 0:32, 63], AF.Exp),
                             r=[kc], w=[("dl", di)]),
            ]

        loads(heads[0])
        P.op("sp", lambda e: e.dma_start(out=g, in_=gT_s[heads[0]]), r=[("proj", 3, heads[0])], w=["g"], slot="hg")
        for hi_, h in enumerate(heads):
            hb = h % 2
            vt = vtm[hb]
            vtO = vtmO[hb]
            for d in range(2):
                P.op("pool", lambda e, d=d: e.memset(S32[d], 0.0), w=[("S32", d)])
                P.op("pool", lambda e, d=d: e.memset(Sbf[d], 0.0), w=[("Sbf", d)])
            kxB, kkB, kcB = ("X", "B"), ("K", "B"), ("C", "B")
            kxA, kkA, kcA = ("X", "A"), ("K", "A"), ("C", "A")
            opsB = chain_ops(XB, KB, CB, E, 1, h, "B") + [
                lambda: P.op("dve", lambda e: e.tensor_tensor(XB, CB, XB, ALU.subtract), r=[kcB, kxB], w=[kxB]),
                lambda: P.op("act", lambda e: e.activation(CB, XB, AF.Exp), r=[kxB], w=[kcB]),
                lambda: P.op("dve", lambda e: e.tensor_tensor(ktB, KB, CB, ALU.mult), r=[kkB, kcB], w=["ktB"]),
                lambda: P.op("act", lambda e: e.activation(CB[:, :OWN], XB[:, :OWN], AF.Exp, scale=-1.0),
                             r=[kxB], w=[kcB]),
                lambda: P.op("dve", lambda e: e.tensor_tensor(qtB, q, CB[:, :OWN], ALU.mult), r=["q", kcB], w=["qtB"]),
            ]
            opsA = chain_ops(XA, KA, CA, OWN, 0, h, "A") + [
                lambda: P.op("act", lambda e: e.activation(XA, CA, AF.Exp), r=[kcA], w=[kxA]),
                lambda: P.op("dve", lambda e: e.tensor_tensor(qtA, q, XA, ALU.mult), r=["q", kxA], w=["qtA"]),
                lambda: P.op("act", lambda e: e.activation(XA, CA, AF.Exp, scale=-1.0), r=[kcA], w=[kxA]),
                lambda: P.op("dve", lambda e: e.tensor_tensor(ktA, KA, XA, ALU.mult), r=[kkA, kxA], w=["ktA"]),
            ]
            for i in range(max(len(opsA), len(opsB))):
                if i < len(opsB):
                    opsB[i]()
                if i < len(opsA):
                    opsA[i]()
            tg = 0
            for grp in range(16):
                tb = 6 + tg % 2
                tg += 1
                ps_tb = bank(tb).bitcast(BF16)
                for i4 in range(4):
                    idx = grp * 4 + i4
                    if idx < 32:
                        src, skey = ktB[:, idx * 64:(idx + 1) * 64], "ktB"
                    else:
                        src, skey = ktA[:, (idx - 32) * 64:(idx - 31) * 64], "ktA"
                    o_ap = ps_tb[0:64, i4 * 128:(i4 + 1) * 128]
                    P.op("pe", lambda e, src=src, o_ap=o_ap: e.transpose(o_ap, src, ident_bf),
                         r=[skey, "ident"], w=[bk(tb)])
                dst = ktm[:, grp * 4:(grp + 1) * 4, :]
                srcp = ps_tb[0:64, 0:512].rearrange("p (n c) -> p n c", c=128)
                if grp % 2 == 0:
                    P.op("act", lambda e, dst=dst, srcp=srcp: e.activation(dst, srcp, AF.Copy),
                         r=[bk(tb)], w=[("ktm", grp)])
                else:
                    P.op("dve", lambda e, dst=dst, srcp=srcp: e.tensor_copy(dst, srcp),
                         r=[bk(tb)], w=[("ktm", grp)])
            for grp in range(4):
                tb = 6 + tg % 2
                tg += 1
                ps_tb = bank(tb).bitcast(BF16)
                for i4 in range(4):
                    i = grp * 4 + i4
                    src = ktB[:, OWN + i * 128:OWN + (i + 1) * 128]
                    o_ap = ps_tb[:, i4 * 128:(i4 + 1) * 128]
                    P.op("pe", lambda e, src=src, o_ap=o_ap: e.transpose(o_ap, src, ident_bf),
                         r=["ktB", "ident"], w=[bk(tb)])
                dst = ktmO[:, grp * 4:(grp + 1) * 4, :]
                srcp = ps_tb[:, 0:512].rearrange("p (n c) -> p n c", c=128)
                if grp % 2 == 0:
                    P.op("act", lambda e, dst=dst, srcp=srcp: e.activation(dst, srcp, AF.Copy),
                         r=[bk(tb)], w=[("ktmO", grp)])
                else:
                    P.op("dve", lambda e, dst=dst, srcp=srcp: e.tensor_copy(dst, srcp),
                         r=[bk(tb)], w=[("ktmO", grp)])
            if hi_ + 1 < len(heads):
                loads(heads[hi_ + 1])
            cnt = dict(sc=0, ds=0)

            def s_update(d, ds_ap, dskey, scale_idx):
                P.op("dve", lambda e: e.tensor_tensor(Ssum[d], ds_ap, S32[d], ALU.add),
                     r=[dskey, ("S32", d)], w=[("Ssum", d)])
                dl_ap = dl[d][:, scale_idx:scale_idx + 1]
                P.op("act", lambda e: e.activation(Sbf[d], Ssum[d], AF.Copy, scale=dl_ap),
                     r=[("Ssum", d), ("dl", d)], w=[("Sbf", d)])
                P.op("dve", lambda e: e.tensor_scalar_mul(S32[d], Ssum[d], dl_ap),
                     r=[("Ssum", d), ("dl", d)], w=[("S32", d)])

            dsO = bank(2)[:, 0:128]
            for i in range(16):
                P.op("pe", lambda e, i=i, vtO=vtO: e.matmul(dsO, ktmO[:, i, :], vtO[:, i, :], start=(i == 0), stop=(i == 15)),
                     r=[("ktmO", i // 4), ("vtmO", hb, i // 8)], w=[bk(2)])
            cnt["ds"] = 1
            s_update(1, dsO, bk(2), 31)

            def sc_part(d, c):
                qt = (qtA, qtB)[d]
                kt = (ktA, ktB)[d]
                qk, kk = ("qtA", "qtB")[d], ("ktA", "ktB")[d]
                sl = cnt["sc"] % 2
                sm = cnt["sc"] % 4
                cnt["sc"] += 1
                sc_ap = bank(sl)[0:64, 0:64]
                P.op("pe", lambda e: e.matmul(sc_ap, kt[:, c * 64:(c + 1) * 64], qt[:, c * 64:(c + 1) * 64],
                                              start=True, stop=True),
                     r=[kk, qk], w=[bk(sl)])
                P.op("dve", lambda e: e.tensor_tensor(scm[sm][0:64, :], sc_ap, tri[0:64, d * 64:(d + 1) * 64],
                                                      ALU.mult),
                     r=[bk(sl), "tri"], w=[("scm", sm)])
                return sm

            def out_part(d, c, kidx, scale_idx, sm, vt=vt, hb=hb):
                qt = (qtA, qtB)[d]
                qk = ("qtA", "qtB")[d]
                vk = ("vtm", hb, c // 16)
                grp = c // 8
                ob_ = bank(4 + d)
                o_ap = ob_[:, (c % 8) * 64:(c % 8 + 1) * 64]
                okey = bk(4 + d)
                P.op("pe", lambda e: e.matmul(o_ap, vt[:, c, :], scm[sm][0:64, :], start=True, stop=False),
                     r=[vk, ("scm", sm)], w=[okey])
                P.op("pe", lambda e: e.matmul(o_ap, Sbf[d], qt[:, c * 64:(c + 1) * 64], start=False, stop=True),
                     r=[("Sbf", d), qk], w=[okey])
                last_in_grp = (c % 8 == 7) if d == 0 else (c % 8 == 0)
                if last_in_grp:
                    od = (oA, oB)[d]
                    P.op("act", lambda e: e.activation(od[:, grp * 512:(grp + 1) * 512], ob_, AF.Copy),
                         r=[okey], w=[("o", d, grp)])
                if scale_idx is not None:
                    ds = 2 + cnt["ds"] % 2
                    cnt["ds"] += 1
                    ds_ap = bank(ds)[:, 0:128]
                    P.op("pe", lambda e: e.matmul(ds_ap, ktm[:, kidx, :], vt[:, c, :], start=True, stop=True),
                         r=[("ktm", kidx // 4), vk], w=[bk(ds)])
                    s_update(d, ds_ap, bk(ds), scale_idx)

            smB = sc_part(1, 31)
            smA = sc_part(0, 0)
            for s in range(32):
                cb = 31 - s
                nxt = None
                if s + 1 < 32:
                    nxt = (sc_part(1, cb - 1), sc_part(0, s + 1))
                out_part(1, cb, cb, (cb - 1) if cb > 0 else None, smB)
                out_part(0, s, 32 + s, s if s < 31 else None, smA)
                if nxt is not None:
                    smB, smA = nxt
            okeys = [("o", d, grp) for d in range(2) for grp in range(4)]
            P.op("dve", lambda e: e.tensor_tensor(oA, oA, oB, ALU.add), r=okeys, w=["osum"])
            P.op("act", lambda e: e.activation(sqh, oA, AF.Square), r=["osum"], w=["sqh"])
            for i in range(4):
                P.op("pe", lambda e, i=i: e.matmul(ps_n, ones_bf, sqh[:, i * 512:(i + 1) * 512], start=True, stop=True),
                     r=["sqh", "ones"], w=[bk(6)])
                rstd_from_bank(ps_n, rstdh[:, i * 512:(i + 1) * 512], 128, "b6")
            P.op("dve", lambda e: e.scalar_tensor_tensor(oA, oA, hn_sb[:, 0:1], rstdh, ALU.mult, ALU.mult),
                 r=["osum", ("rstd", "b6"), "hn"], w=["osum"] + okeys)
            P.op("dve", lambda e: e.tensor_tensor(ob, oA, g, ALU.mult), r=["osum", "g"], w=["ob"])
            if hi_ + 1 < len(heads):
                hn_ = heads[hi_ + 1]
                P.op("sp", lambda e, hn_=hn_: e.dma_start(out=g, in_=gT_s[hn_]), r=[("proj", 3, hn_)], w=["g"],
                     slot="hg")
            P.op("sp", lambda e, h=h: e.dma_start(out=mixT_s[h * 128:(h + 1) * 128, :], in_=ob),
                 r=["ob"], w=[("mixT", h)], slot="hout")

    def na_phase(heads=range(NH)):
        cv = Carver()
        cv.off = const_end
        nq = cv.bf16(OWN)
        nk = cv.bf16(OWN + T)
        nvt = cv.bf16(20 * 128).rearrange("p (n c) -> p n c", c=128)
        strip_sb = cv.bf16(1408)
        mask_sb = cv.bf16(16 * 512)
        Pb = [cv.bf16(512) for _ in range(3)]
        rec = cv.f32(512)
        ost = [cv.bf16(512) for _ in range(2)]
        ps_s = [bank(0), bank(1)]
        ps_o = [bank(2), bank(3)]
        ps_d = [bank(4), bank(5)]
        P.op("pool", lambda e: e.dma_start(out=mask_sb.rearrange("p (n t) -> p n t", t=512),
                                           in_=maskd.rearrange("n p t -> p n t")), w=["mask"], slot="nmask")
        qbc = 0
        for h in heads:
            P.op("sp", lambda e, h=h: e.dma_start(out=nq, in_=nqT_s[h]), r=[("proj", 4, h)], w=["nq"], slot="nq")
            P.op("sp", lambda e, h=h: e.dma_start(out=nk, in_=nkT_s[h]), r=[("proj", 5, h)], w=["nk"], slot="nk")
            nsrc = nv_s[:, h * 128:(h + 1) * 128].rearrange("(n p) c -> p n c", p=128)
            for i in range(2):
                P.op("sp", lambda e, i=i, nsrc=nsrc: e.dma_start(out=nvt[:, i * 10:(i + 1) * 10, :],
                                                                 in_=nsrc[:, i * 10:(i + 1) * 10, :]),
                     r=[("nv_s", 4 + h // 4)], w=[("nvt", i)], slot="nv%d" % i)
            P.op("pool", lambda e, h=h: e.dma_start(out=strip_sb, in_=strip[h]), w=["strip"], slot="nstrip")
            for qb in range(4):
                tiles = [kt for kt in range(4 * qb - 2, 4 * qb + 6) if kt >= 0]
                mt = 0 if qb == 0 else 1
                po = qbc % 2
                qbc += 1

                def s_mm(i, kt, qb=qb, mt=mt):
                    delta = 2 * kt - 8 * qb
                    di = (delta + 4) // 2
                    off = (10 - delta) * 64
                    b = i % 2
                    P.op("pe", lambda e: e.matmul(ps_s[b], nk[:, kt * 128:(kt + 1) * 128],
                                                  nq[:, qb * 512:(qb + 1) * 512], start=True, stop=False),
                         r=["nk", "nq"], w=[("ps", "s", b)])
                    P.op("pe", lambda e: e.matmul(ps_s[b], ident_bf, strip_sb[:, off:off + 512], start=False,
                                                  stop=False),
                         r=["ident", "strip"], w=[("ps", "s", b)])
                    P.op("pe", lambda e: e.matmul(ps_s[b], ident_bf,
                                                  mask_sb[:, (mt * 8 + di) * 512:(mt * 8 + di + 1) * 512],
                                                  start=False, stop=True),
                         r=["ident", "mask"], w=[("ps", "s", b)])

                s_mm(0, tiles[0])
                for i, kt in enumerate(tiles):
                    if i + 1 < len(tiles):
                        s_mm(i + 1, tiles[i + 1])
                    b = i % 2
                    pb = i % 3
                    P.op("act", lambda e, b=b, pb=pb: e.activation(Pb[pb], ps_s[b], AF.Exp),
                         r=[("ps", "s", b)], w=[("Pb", pb)])
                    first, last = (i == 0), (i == len(tiles) - 1)
                    P.op("pe", lambda e, kt=kt, pb=pb, first=first, last=last, po=po: e.matmul(
                        ps_o[po], nvt[:, kt, :], Pb[pb], start=first, stop=last),
                        r=[("nvt", kt // 10), ("Pb", pb)], w=[("ps", "no", po)])
                    P.op("pe", lambda e, pb=pb, first=first, last=last, po=po: e.matmul(
                        ps_d[po], ones_bf, Pb[pb], start=first, stop=last),
                        r=["ones", ("Pb", pb)], w=[("ps", "nd", po)])
                P.op("dve", lambda e, po=po: e.reciprocal(rec, ps_d[po]), r=[("ps", "nd", po)], w=["rec"])
                P.op("dve", lambda e, po=po: e.tensor_tensor(ost[po], ps_o[po], rec, ALU.mult),
                     r=[("ps", "no", po), "rec"], w=[("nost", po)])
                P.op("sp", lambda e, po=po, h=h, qb=qb: e.dma_start(
                    out=mixT_s[2048 + h * 128:2048 + (h + 1) * 128, qb * 512:(qb + 1) * 512], in_=ost[po]),
                    r=[("nost", po)], w=[("mixT", 16 + h)], slot="no%d" % po)

    def wout_phase(n_tiles=4):
        cv = Carver()
        cv.off = const_end
        mxb = [cv.bf16(NC_ * T).rearrange("p (c t) -> p c t", t=T) for _ in range(2)]
        wb = [cv.bf16(NC_ * 128) for _ in range(3)]
        ffst = [cv.f32(T) for _ in range(2)]
        sqF = [cv.bf16(T) for _ in range(2)]
        rstdF = cv.f32(T)
        stF = [cv.f32(T) for _ in range(2)]
        stR = [cv.f32(T) for _ in range(2)]
        ps_o = [bank(4), bank(5)]
        ps_n2 = bank(7)
        mkeys = [("mixT", i) for i in range(32)]

        def load_mx(ti):
            mb = ti % 2
            msrc = mixT_s[:, ti * T:(ti + 1) * T].rearrange("(c p) t -> p c t", p=128)
            for i in range(4):
                P.op("sp", lambda e, i=i: e.dma_start(out=mxb[mb][:, i * 8:(i + 1) * 8, :],
                                                      in_=msrc[:, i * 8:(i + 1) * 8, :]),
                     r=mkeys, w=[("mx", mb, i)], slot="mx%d_%d" % (mb, i))

        def fin_steps(col0):
            steps = [[] for _ in range(36)]

            def fa(m):
                sa, sb = stF[m % 2], stR[m % 2]
                P.op("sp", lambda e: e.dma_start(out=sa, in_=ffT[m * 128:(m + 1) * 128, col0:col0 + T]),
                     r=[("ffT", m)], w=[("stF", m % 2)], slot="sfa%d" % (m % 2))
                P.op("sp", lambda e: e.dma_start(out=sb, in_=h1T[m * 128:(m + 1) * 128, col0:col0 + T]),
                     r=["h1T"], w=[("stR", m % 2)], slot="sfr%d" % (m % 2))

            def fb(m):
                sa, sb = stF[m % 2], stR[m % 2]
                P.op("dve", lambda e: e.scalar_tensor_tensor(sa, sa, gain_ap(3, m), rstdF, ALU.mult, ALU.mult),
                     r=[("stF", m % 2), ("rstd", "n2"), "gains"], w=[("stF", m % 2)])
                P.op("dve", lambda e: e.tensor_tensor(sa, sa, sb, ALU.add),
                     r=[("stF", m % 2), ("stR", m % 2)], w=[("stF", m % 2)])
                P.op("sp", lambda e: e.dma_start(out=h2T[m * 128:(m + 1) * 128, col0:col0 + T], in_=sa),
                     r=[("stF", m % 2)], w=["h2T"], slot="sfo%d" % (m % 2))

            for m in range(NC_):
                steps[m].append(lambda m=m: fa(m))
                steps[m + 1].append(lambda m=m: fb(m))
            return steps

        load_mx(0)
        pend = None
        wk = 0
        for ti in range(n_tiles):
            col0 = ti * T
            mx = mxb[ti % 2]
            mb = ti % 2
            if ti + 1 < n_tiles:
                load_mx(ti + 1)
            for m in range(NC_):
                k = wk % 3
                wk += 1
                pb = m % 2
                P.op("pool", lambda e, k=k, m=m: e.dma_start(out=wb[k], in_=wout[m]), w=[("wb", k)], slot="wb%d" % k)
                for c in range(NC_):
                    P.op("pe", lambda e, k=k, c=c, pb=pb, mx=mx: e.matmul(ps_o[pb], wb[k][:, c * 128:(c + 1) * 128],
                                                                        mx[:, c, :], start=(c == 0),
                                                                        stop=(c == NC_ - 1)),
                         r=[("wb", k), ("mx", mb, c // 8)], w=[("ps", "o", pb)])
                ff_chunk_out(ps_o[pb], ("ps", "o", pb), m, col0, ffst, sqF, ps_n2)
                if pend is not None:
                    for th in pend[m]:
                        th()
            if pend is not None:
                for k2 in range(NC_, len(pend)):
                    for th in pend[k2]:
                        th()
            rstd_from_bank(ps_n2, rstdF, D, "n2")
            pend = fin_steps(col0)
        for st_l in pend:
            for th in st_l:
                th()

    if phases >= 2:
        inproj_phase()
        P.barrier()
    if phases >= 3:
        hgrn_phase()
        P.barrier()
    if phases >= 4:
        na_phase()
        P.barrier()
    if phases >= 5:
        wout_phase()
        P.barrier()
    if phases >= 6:
        ffn_phase(1, h2T, "h2T", [i * T for i in range(4)], 4, 5, outT, "outT", [i * T for i in range(4)])

    P.barrier()
    P.emit(stack)
    stack.close()
    return nc


def _tile_fm(w, ncols_chunk):
    K, n = w.shape
    g = n // ncols_chunk
    return np.ascontiguousarray(
        w.reshape(K // 128, 128, g, ncols_chunk).transpose(2, 1, 0, 3)).reshape(g, 128, (K // 128) * ncols_chunk)


def _tile_down(w):
    a = w.reshape(2, 43, 128, 32, 128).transpose(3, 0, 2, 1, 4)
    return np.ascontiguousarray(a).reshape(32, 2, 128, 43 * 128)


def _na_tables(rpb, parity):
    def true_row(lr):
        return lr if parity == 0 else 63 - lr

    def true_col(lc):
        return lc if parity == 0 else 63 - lc

    sgn = 1 if parity == 0 else -1
    j = np.arange(2)[:, None, None, None]
    kc = np.arange(64)[None, :, None, None]
    m = (np.arange(22) - 10)[None, None, :, None]
    qc = np.arange(64)[None, None, None, :]
    dr = sgn * (j - m) + 7 + 0 * kc + 0 * qc
    dc = np.clip(sgn * (kc - qc) + 15, 0, 30) + 0 * j + 0 * m
    valid = (dr >= 0) & (dr <= 14)
    drc = np.clip(dr, 0, 14)
    strip = rpb[:, drc, dc]
    strip = np.where(valid[None], strip, np.float32(0.0)).astype(np.float32)
    strip = np.ascontiguousarray(strip).reshape(16, 128, 22 * 64)
    masks = np.zeros((2, 8, 128, 512), np.float32)
    for mt, qb in enumerate((0, 1)):
        for di in range(8):
            delta = 2 * di - 4
            kt = (8 * qb + delta) // 2
            for jj in range(2):
                kr_l = 2 * kt + jj
                for i in range(8):
                    qr_l = 8 * qb + i
                    if kr_l < 0:
                        masks[mt, di, jj * 64:(jj + 1) * 64, i * 64:(i + 1) * 64] = NEG
                        continue
                    r = true_row(qr_l)
                    kr = true_row(kr_l)
                    rs = min(max(r - 4, 0), 56)
                    row_ok = (rs <= kr < rs + 8)
                    lc = np.arange(64)
                    c = true_col(lc)[None, :]
                    kcol = true_col(lc)[:, None]
                    cs = np.clip(c - 8, 0, 48)
                    col_ok = (kcol >= cs) & (kcol < cs + 16)
                    ok = col_ok & row_ok
                    masks[mt, di, jj * 64:(jj + 1) * 64, i * 64:(i + 1) * 64] = np.where(ok, 0.0, NEG)
    return strip, masks.reshape(16, 128, 512)


def prepare_inputs(inputs):
    f = lambda k: np.asarray(inputs[k], dtype=np.float32)
    x = f("x")
    shared = {}
    for n, k in ((1, "ffn1"), (2, "ffn2")):
        shared["wg%d" % n] = _tile_fm(f(k + "_w_gate")[0], 128)
        shared["wu%d" % n] = _tile_fm(f(k + "_w_up")[0], 128)
        shared["wd%d" % n] = _tile_down(f(k + "_w_down")[0])
    shared["wout"] = _tile_fm(f("w_out")[0], 128)
    gl = [f(k)[0] for k in ("ffn1_norm_pre", "ffn1_norm_post", "mix_norm_pre", "mix_norm_post",
                            "ffn2_norm_pre", "ffn2_norm_post")]
    shared["gains"] = np.ascontiguousarray(np.concatenate([g.reshape(32, 128).T for g in gl], axis=1))
    shared["hn"] = np.ascontiguousarray(f("hgrn_head_norm")[0].reshape(128, 1))
    shared["ident"] = np.eye(128, dtype=np.float32)
    tri = np.zeros((64, 2, 64), np.float32)
    s = np.arange(64)[:, None]
    t = np.arange(64)[None, :]
    tri[:, 0, :] = (s <= t)
    tri[:, 1, :] = (s >= t)
    shared["tri"] = tri.reshape(64, 128)
    w_in = f("w_in")[0]
    blk = lambda i: w_in[:, i * 2048:(i + 1) * 2048]
    shared["win_tm"] = np.concatenate([_tile_fm(blk(3), 512), _tile_fm(blk(7), 512)], axis=0)
    lbl = f("hgrn_lb_logits")
    rpb = f("na_rpb")[0]
    par = []
    for parity in range(2):
        fa, fb = (1, 2) if parity == 0 else (2, 1)
        d = {}
        d["win_fm"] = np.concatenate([_tile_fm(blk(i), 128) for i in (0, fa, fb, 4, 5, 6)], axis=0)
        dirs = (0, 1) if parity == 0 else (1, 0)
        l4 = lbl[:, dirs, :].reshape(2, 2, 16, 128).transpose(3, 0, 1, 2)
        d["lbl"] = np.ascontiguousarray(l4).reshape(128, 64)
        d["strip"], d["mask"] = _na_tables(rpb, parity)
        par.append(d)
    maps = []
    for core in range(8):
        b, half = core // 2, core % 2
        xl = x[b] if half == 0 else x[b, ::-1]
        m = dict(shared)
        m.update(par[half])
        m["xT"] = np.ascontiguousarray(xl.T)
        maps.append(m)
    return maps


def assemble_output(res_list):
    out = np.empty((4, 4096, D), np.float32)
    for core in range(8):
        b, half = core // 2, core % 2
        o = np.asarray(res_list[core]["outT"]).T
        if half == 0:
            out[b, :OWN] = o
        else:
            out[b, OWN:] = o[::-1]
    return out


def kernel(**inputs):
    maps = prepare_inputs(inputs)
    nc = build_program()
    res = run_bass_kernel_spmd(nc, maps, core_ids=list(range(8)))
    return assemble_output(res.results)
```
